# Optimizing a Trainium2 kernel written in Bass

```python
import math
import jax, jax.numpy as jnp
from jax import lax
import numpy as np

D_MODEL = 2048
BATCH = 4
SEQ = 2048
DEPTH = 4
DEC_BATCH = 128
DEC_SEQ = 4
PAST_LEN = 16384
PAGE_SIZE = 128

N_MIXERS = 3
N_HGRN = (DEPTH + 2) // 3
N_MLSTM = (DEPTH + 1) // 3
N_GMLP = DEPTH // 3

HG_DK = 128
HG_HEADS = D_MODEL // HG_DK
HG_DV = D_MODEL // HG_HEADS
HG_CHUNK = 64

ML_HEADS = 8
ML_DQK = D_MODEL // (2 * ML_HEADS)
ML_DV = D_MODEL // ML_HEADS
ML_CHUNK = 64
GATE_CAP = 15.0

GM_DIM = D_MODEL
GM_CHUNK = 128
GM_GROUPS = 16
GM_GDIM = GM_DIM // GM_GROUPS

FFN_DIM = 256 * ((8 * D_MODEL // 3 + 255) // 256)
CONV_W = 3
EPS = 1e-6

kernel_name = 'hybrid_hgrn2_mlstm_gmlp_convffn_step'


def _rmsnorm(x, g):
    xf = x.astype(jnp.float32)
    y = xf * lax.rsqrt(jnp.mean(xf * xf, axis=-1, keepdims=True) + EPS)
    return (y * g.astype(jnp.float32)).astype(x.dtype)


def _layernorm(x, g, b):
    xf = x.astype(jnp.float32)
    mu = jnp.mean(xf, axis=-1, keepdims=True)
    xc = xf - mu
    y = xc * lax.rsqrt(jnp.mean(xc * xc, axis=-1, keepdims=True) + EPS)
    return (y * g.astype(jnp.float32) + b.astype(jnp.float32)).astype(x.dtype)


def _chunk_len(T, c):
    return T if T <= c else c


def _to_chunks(a, L):
    B, T, H, d = a.shape
    return a.reshape(B, T // L, L, H, d).transpose(1, 0, 3, 2, 4)


def _gates_to_chunks(a, L):
    B, T, H = a.shape
    return a.reshape(B, T // L, L, H).transpose(1, 0, 3, 2)


def _from_chunks(a):
    n, B, H, L, d = a.shape
    return a.transpose(1, 0, 3, 2, 4).reshape(B, n * L, H, d)


def _hgrn_mixer(h, S0, w_q, w_f, w_i, w_g, lb, onorm, w_o):
    B, T, _ = h.shape
    f32 = jnp.float32
    q = jax.nn.silu(h @ w_q).astype(f32).reshape(B, T, HG_HEADS, HG_DK)
    fpre = (h @ w_f).astype(f32).reshape(B, T, HG_HEADS, HG_DK)
    v = (h @ w_i).astype(f32).reshape(B, T, HG_HEADS, HG_DV)
    lbh = jnp.maximum(lb.astype(f32), 0.0).reshape(HG_HEADS, HG_DK)
    logf = jnp.logaddexp(jnp.log(lbh), jnp.log1p(-lbh) + jax.nn.log_sigmoid(fpre))
    k = (1.0 - lbh) * jax.nn.sigmoid(-fpre)
    L = _chunk_len(T, HG_CHUNK)
    tri = jnp.tril(jnp.ones((L, L), dtype=bool))

    def step(S, xs):
        qc, kc, vc, lfc = xs
        b = jnp.cumsum(lfc, axis=2)
        o_inter = jnp.einsum('bhtk,bhkv->bhtv', qc * jnp.exp(b), S)
        diff = b[:, :, :, None, :] - b[:, :, None, :, :]
        decay = jnp.exp(jnp.where(tri[:, :, None], diff, -jnp.inf))
        A = jnp.einsum('bhtk,bhsk,bhtsk->bhts', qc, kc, decay)
        o = o_inter + jnp.einsum('bhts,bhsv->bhtv', A, vc)
        bL = b[:, :, -1:, :]
        S = jnp.exp(bL[:, :, 0])[..., None] * S + jnp.einsum('bhsk,bhsv->bhkv', kc * jnp.exp(bL - b), vc)
        return S, o

    S, o = lax.scan(step, S0.astype(f32), (_to_chunks(q, L), _to_chunks(k, L), _to_chunks(v, L), _to_chunks(logf, L)))
    o = _from_chunks(o).reshape(B, T, HG_HEADS * HG_DV)
    o = _rmsnorm(o, onorm).astype(h.dtype) * jax.nn.sigmoid(h @ w_g)
    return o @ w_o, S


def _mlstm_mixer(h, C0, n0, m0, w_q, w_k, w_v, w_og, w_if, b_if, hnorm, w_out):
    B, T, _ = h.shape
    f32 = jnp.float32
    q = (h @ w_q).astype(f32).reshape(B, T, ML_HEADS, ML_DQK)
    k = (h @ w_k).astype(f32).reshape(B, T, ML_HEADS, ML_DQK) * (ML_DQK ** -0.5)
    v = (h @ w_v).astype(f32).reshape(B, T, ML_HEADS, ML_DV)
    gates = GATE_CAP * jnp.tanh((h @ w_if + b_if).astype(f32) / GATE_CAP)
    ig = gates[..., :ML_HEADS]
    lf = jax.nn.log_sigmoid(gates[..., ML_HEADS:])
    L = _chunk_len(T, ML_CHUNK)
    tri = jnp.tril(jnp.ones((L, L), dtype=bool))

    def step(carry, xs):
        C, n, m = carry
        qc, kc, vc, igc, lfc = xs
        b = jnp.cumsum(lfc, axis=-1)
        dlog = jnp.where(tri, b[..., :, None] - b[..., None, :] + igc[..., None, :], -jnp.inf)
        inter = b + m[..., None]
        mt = jnp.maximum(inter, jnp.max(dlog, axis=-1))
        wts = jnp.exp(dlog - mt[..., None]) * jnp.einsum('bhtd,bhsd->bhts', qc, kc)
        sc = jnp.exp(inter - mt)
        num = sc[..., None] * jnp.einsum('bhtd,bhdv->bhtv', qc, C) + jnp.einsum('bhts,bhsv->bhtv', wts, vc)
        den = sc * jnp.einsum('bhtd,bhd->bht', qc, n) + jnp.sum(wts, axis=-1)
        out = num / jnp.maximum(jnp.abs(den), jnp.exp(-mt))[..., None]
        mL = mt[..., -1]
        sc_state = jnp.exp(b[..., -1] + m - mL)
        wk = jnp.exp(b[..., -1:] - b + igc - mL[..., None])
        C = sc_state[..., None, None] * C + jnp.einsum('bhs,bhsd,bhsv->bhdv', wk, kc, vc)
        n = sc_state[..., None] * n + jnp.einsum('bhs,bhsd->bhd', wk, kc)
        return (C, n, mL), out

    (C, n, m), o = lax.scan(step, (C0.astype(f32), n0.astype(f32), m0.astype(f32)),
                            (_to_chunks(q, L), _to_chunks(k, L), _to_chunks(v, L),
                             _gates_to_chunks(ig, L), _gates_to_chunks(lf, L)))
    o = _from_chunks(o)
    o = _rmsnorm(o, hnorm.reshape(ML_HEADS, ML_DV)).reshape(B, T, ML_HEADS * ML_DV)
    o = o.astype(h.dtype) * jax.nn.sigmoid(h @ w_og)
    return o @ w_out, (C, n, m)


def _gmlp_mixer(h, w_in, b_in, vg, vb, w_s, b_s, w_out):
    B, T, _ = h.shape
    z = jax.nn.gelu(h @ w_in + b_in, approximate=False)
    u, v = jnp.split(z, 2, axis=-1)
    v = _layernorm(v, vg, vb)
    L = _chunk_len(T, GM_CHUNK)
    tri = jnp.tril(jnp.ones((L, L), dtype=bool))
    ws = jnp.where(tri, w_s[:, :L, :L], 0.0)
    vc = v.reshape(B, T // L, L, GM_GROUPS, GM_GDIM)
    mix = jnp.einsum('gts,bnsgd->bntgd', ws, vc) + b_s[:, :L].T[None, None, :, :, None]
    o = u * mix.reshape(B, T, GM_DIM)
    return o @ w_out, v


def _conv_ffn(h, buf, w_up, cw, cb, w_down):
    T = h.shape[1]
    up = h @ w_up
    hp = jnp.concatenate([buf.astype(up.dtype), up], axis=1)
    y = cb
    for j in range(CONV_W):
        y = y + cw[j] * hp[:, j:j + T]
    a, g = jnp.split(y, 2, axis=-1)
    return (a * jax.nn.silu(g)) @ w_down, hp[:, T:]


def _trunk(x, S_in, C_in, n_in, m_in, conv_in, p):
    lb = jax.nn.softmax(p['hgrn_lb'].astype(jnp.float32), axis=0)
    lbs = jnp.cumsum(lb, axis=0) - lb[0]
    new_S, new_C, new_n, new_m, new_v, new_conv = [], [], [], [], [], []
    for i in range(DEPTH):
        j = i // N_MIXERS
        kind = i % N_MIXERS
        h = _rmsnorm(x, p['norm_mix'][i])
        if kind == 0:
            o, S = _hgrn_mixer(h, S_in[j], p['hgrn_w_q'][j], p['hgrn_w_f'][j], p['hgrn_w_i'][j],
                               p['hgrn_w_g'][j], lbs[j], p['hgrn_onorm'][j], p['hgrn_w_o'][j])
            new_S.append(S)
        elif kind == 1:
            o, (C, n, m) = _mlstm_mixer(h, C_in[j], n_in[j], m_in[j], p['mlstm_w_q'][j], p['mlstm_w_k'][j],
                                        p['mlstm_w_v'][j], p['mlstm_w_og'][j], p['mlstm_w_if'][j],
                                        p['mlstm_b_if'][j], p['mlstm_hnorm'][j], p['mlstm_w_out'][j])
            new_C.append(C)
            new_n.append(n)
            new_m.append(m)
        else:
            o, v = _gmlp_mixer(h, p['gmlp_w_in'][j], p['gmlp_b_in'][j], p['gmlp_vnorm_g'][j],
                               p['gmlp_vnorm_b'][j], p['gmlp_w_s'][j], p['gmlp_b_s'][j], p['gmlp_w_out'][j])
            new_v.append(v)
        x = x + o
        h = _rmsnorm(x, p['norm_ffn'][i])
        o, buf = _conv_ffn(h, conv_in[i], p['ffn_w_up'][i], p['ffn_conv_w'][i], p['ffn_conv_b'][i], p['ffn_w_down'][i])
        new_conv.append(buf)
        x = x + o
    y = _rmsnorm(x, p['norm_final'])
    return (y, jnp.stack(new_S), jnp.stack(new_C), jnp.stack(new_n), jnp.stack(new_m),
            jnp.stack(new_v), jnp.stack(new_conv))


def setup_inputs(seed: int = 0) -> dict:
    key = jax.random.key(seed)
    ks = iter(jax.random.split(key, 48))

    def nrm(shape, scale):
        return scale * jax.random.normal(next(ks), shape, jnp.float32)

    F2 = 2 * FFN_DIM
    res = (2 * DEPTH) ** -0.5
    dn = D_MODEL ** -0.5
    return {
        'x_prompt': nrm((BATCH, SEQ, D_MODEL), 1.0),
        'x_sample': nrm((DEC_BATCH, DEC_SEQ, D_MODEL), 1.0),
        'state_hgrn_S': nrm((N_HGRN, DEC_BATCH, HG_HEADS, HG_DK, HG_DV), 0.5),
        'state_mlstm_C': nrm((N_MLSTM, DEC_BATCH, ML_HEADS, ML_DQK, ML_DV), 0.3),
        'state_mlstm_n': nrm((N_MLSTM, DEC_BATCH, ML_HEADS, ML_DQK), 0.3),
        'state_mlstm_m': nrm((N_MLSTM, DEC_BATCH, ML_HEADS), 1.0),
        'state_ffn_conv': nrm((DEPTH, DEC_BATCH, CONV_W - 1, F2), 1.0),
        'norm_mix': 1.0 + nrm((DEPTH, D_MODEL), 0.02),
        'norm_ffn': 1.0 + nrm((DEPTH, D_MODEL), 0.02),
        'norm_final': 1.0 + nrm((D_MODEL,), 0.02),
        'hgrn_w_q': nrm((N_HGRN, D_MODEL, HG_HEADS * HG_DK), dn),
        'hgrn_w_f': nrm((N_HGRN, D_MODEL, HG_HEADS * HG_DK), dn),
        'hgrn_w_i': nrm((N_HGRN, D_MODEL, HG_HEADS * HG_DV), dn),
        'hgrn_w_g': nrm((N_HGRN, D_MODEL, HG_HEADS * HG_DV), dn),
        'hgrn_lb': nrm((N_HGRN, HG_HEADS * HG_DK), 0.5),
        'hgrn_onorm': 1.0 + nrm((N_HGRN, HG_HEADS * HG_DV), 0.02),
        'hgrn_w_o': nrm((N_HGRN, HG_HEADS * HG_DV, D_MODEL), (HG_HEADS * HG_DV) ** -0.5 * res),
        'mlstm_w_q': nrm((N_MLSTM, D_MODEL, ML_HEADS * ML_DQK), dn),
        'mlstm_w_k': nrm((N_MLSTM, D_MODEL, ML_HEADS * ML_DQK), dn),
        'mlstm_w_v': nrm((N_MLSTM, D_MODEL, ML_HEADS * ML_DV), dn),
        'mlstm_w_og': nrm((N_MLSTM, D_MODEL, ML_HEADS * ML_DV), dn),
        'mlstm_w_if': nrm((N_MLSTM, D_MODEL, 2 * ML_HEADS), dn),
        'mlstm_b_if': jnp.concatenate([nrm((N_MLSTM, ML_HEADS), 0.1) - 1.0,
                                       3.0 + nrm((N_MLSTM, ML_HEADS), 0.5)], axis=-1),
        'mlstm_hnorm': 1.0 + nrm((N_MLSTM, ML_HEADS * ML_DV), 0.02),
        'mlstm_w_out': nrm((N_MLSTM, ML_HEADS * ML_DV, D_MODEL), (ML_HEADS * ML_DV) ** -0.5 * res),
        'gmlp_w_in': nrm((N_GMLP, D_MODEL, 2 * GM_DIM), dn),
        'gmlp_b_in': nrm((N_GMLP, 2 * GM_DIM), 0.02),
        'gmlp_vnorm_g': 1.0 + nrm((N_GMLP, GM_DIM), 0.02),
        'gmlp_vnorm_b': nrm((N_GMLP, GM_DIM), 0.02),
        'gmlp_w_s': nrm((N_GMLP, GM_GROUPS, GM_CHUNK, GM_CHUNK), GM_CHUNK ** -0.5),
        'gmlp_b_s': 1.0 + nrm((N_GMLP, GM_GROUPS, GM_CHUNK), 0.1),
        'gmlp_w_out': nrm((N_GMLP, GM_DIM, D_MODEL), GM_DIM ** -0.5 * res),
        'ffn_w_up': nrm((DEPTH, D_MODEL, F2), dn),
        'ffn_conv_w': nrm((DEPTH, CONV_W, F2), 0.3).at[:, CONV_W - 1].add(1.0),
        'ffn_conv_b': nrm((DEPTH, F2), 0.02),
        'ffn_w_down': nrm((DEPTH, FFN_DIM, D_MODEL), FFN_DIM ** -0.5 * res),
    }


def reference(x_prompt, x_sample, state_hgrn_S, state_mlstm_C, state_mlstm_n, state_mlstm_m, state_ffn_conv,
              norm_mix, norm_ffn, norm_final,
              hgrn_w_q, hgrn_w_f, hgrn_w_i, hgrn_w_g, hgrn_lb, hgrn_onorm, hgrn_w_o,
              mlstm_w_q, mlstm_w_k, mlstm_w_v, mlstm_w_og, mlstm_w_if, mlstm_b_if, mlstm_hnorm, mlstm_w_out,
              gmlp_w_in, gmlp_b_in, gmlp_vnorm_g, gmlp_vnorm_b, gmlp_w_s, gmlp_b_s, gmlp_w_out,
              ffn_w_up, ffn_conv_w, ffn_conv_b, ffn_w_down):
    p = dict(norm_mix=norm_mix, norm_ffn=norm_ffn, norm_final=norm_final,
             hgrn_w_q=hgrn_w_q, hgrn_w_f=hgrn_w_f, hgrn_w_i=hgrn_w_i, hgrn_w_g=hgrn_w_g,
             hgrn_lb=hgrn_lb, hgrn_onorm=hgrn_onorm, hgrn_w_o=hgrn_w_o,
             mlstm_w_q=mlstm_w_q, mlstm_w_k=mlstm_w_k, mlstm_w_v=mlstm_w_v, mlstm_w_og=mlstm_w_og,
             mlstm_w_if=mlstm_w_if, mlstm_b_if=mlstm_b_if, mlstm_hnorm=mlstm_hnorm, mlstm_w_out=mlstm_w_out,
             gmlp_w_in=gmlp_w_in, gmlp_b_in=gmlp_b_in, gmlp_vnorm_g=gmlp_vnorm_g, gmlp_vnorm_b=gmlp_vnorm_b,
             gmlp_w_s=gmlp_w_s, gmlp_b_s=gmlp_b_s, gmlp_w_out=gmlp_w_out,
             ffn_w_up=ffn_w_up, ffn_conv_w=ffn_conv_w, ffn_conv_b=ffn_conv_b, ffn_w_down=ffn_w_down)
    f32 = jnp.float32
    bp = x_prompt.shape[0]
    zS = jnp.zeros((N_HGRN, bp, HG_HEADS, HG_DK, HG_DV), f32)
    zC = jnp.zeros((N_MLSTM, bp, ML_HEADS, ML_DQK, ML_DV), f32)
    zn = jnp.zeros((N_MLSTM, bp, ML_HEADS, ML_DQK), f32)
    zm = jnp.zeros((N_MLSTM, bp, ML_HEADS), f32)
    zconv = jnp.zeros((DEPTH, bp, CONV_W - 1, 2 * FFN_DIM), x_prompt.dtype)
    y_prompt, S_p, C_p, n_p, m_p, _, conv_p = _trunk(x_prompt, zS, zC, zn, zm, zconv, p)
    y_sample, S_s, C_s, n_s, m_s, v_s, conv_s = _trunk(x_sample, state_hgrn_S, state_mlstm_C, state_mlstm_n,
                                                       state_mlstm_m, state_ffn_conv, p)
    return (y_prompt, y_sample, S_p, S_s, C_p, C_s, n_p, n_s, m_p, m_s, v_s, conv_p, conv_s)
```

```python
from contextlib import ExitStack
import numpy as np
import concourse.bass as bass
import concourse.mybir as mybir
from concourse.bass_utils import run_bass_kernel_spmd

F32 = mybir.dt.float32
BF16 = mybir.dt.bfloat16
AF = mybir.ActivationFunctionType
ALU = mybir.AluOpType

EPOCH = 1000000000


class Res:
    __slots__ = ("lw", "rd")

    def __init__(self):
        self.lw = None
        self.rd = []


class Buf:
    def __init__(self, t, parts):
        self.t = t
        self.parts = [Res() for _ in range(parts)]

    @property
    def all(self):
        return list(self.parts)

    def __getitem__(self, i):
        return [self.parts[i]]


class Prog:
    ENGS = ("pe", "act", "dve", "pool", "sync")

    def __init__(self, nc, n_dma_sems=12):
        self.nc = nc
        self.stack = ExitStack()
        self.streams = {e: [] for e in self.ENGS}
        self.count = {e: 0 for e in self.ENGS}
        self.n_dma_sems = n_dma_sems
        self.dq = ("sync", "pool", "act")
        self.dma_last = {q: [None] * n_dma_sems for q in self.dq}
        self.dma_val = {q: [0] * n_dma_sems for q in self.dq}
        self.dma_next = {q: 0 for q in self.dq}
        self.out_tokens = []

    def sbuf(self, name, shape, dtype, parts=1):
        t = self.stack.enter_context(self.nc.sbuf_tensor(name, list(shape), dtype))
        return Buf(t, parts)

    def psum(self, name, shape, dtype=F32, parts=1):
        t = self.stack.enter_context(self.nc.psum_tensor(name, list(shape), dtype))
        return Buf(t, parts)

    def _deps(self, reads, writes):
        deps = []
        for r in reads:
            if r.lw is not None:
                deps.append(r.lw)
        for w in writes:
            if w.lw is not None:
                deps.append(w.lw)
            deps.extend(w.rd)
        return deps

    def _commit(self, tok, reads, writes):
        for r in reads:
            r.rd.append(tok)
        for w in writes:
            w.lw = tok
            w.rd = []

    def op(self, eng, fn, reads=(), writes=()):
        reads = list(reads); writes = list(writes)
        deps = self._deps(reads, writes)
        self.count[eng] += 1
        tok = ("e", eng, self.count[eng])
        self.streams[eng].append((deps, fn, tok))
        self._commit(tok, reads, writes)
        return tok

    def dma(self, q, out, in_, reads=(), writes=(), is_output=False):
        reads = list(reads); writes = list(writes)
        deps = self._deps(reads, writes)
        slot = self.dma_next[q]
        self.dma_next[q] = (slot + 1) % self.n_dma_sems
        if self.dma_last[q][slot] is not None:
            deps.append(self.dma_last[q][slot])
        self.dma_val[q][slot] += 16
        tok = ("d", (q, slot), self.dma_val[q][slot])
        self.dma_last[q][slot] = tok
        fn = lambda e, out=out, in_=in_: e.dma_start(out=out, in_=in_)
        self.streams[q].append((deps, fn, tok))
        self._commit(tok, reads, writes)
        if is_output:
            self.out_tokens.append(tok)
        return tok

    def barrier(self):
        toks = []
        for e in ("pe", "act", "dve", "pool"):
            if self.count[e] > 0:
                toks.append(("e", e, self.count[e]))
        for q in self.dq:
            for t in self.dma_last[q]:
                if t is not None:
                    toks.append(t)
        for e in self.ENGS:
            self.streams[e].append((list(toks), None, None))

    def emit(self, same_engine_sync=True):
        nc = self.nc
        st = self.stack
        esems = {}
        for e in ("pe", "act", "dve", "pool"):
            n_ep = self.count[e] // EPOCH + 1
            esems[e] = [st.enter_context(nc.semaphore(f"s_{e}{i}")) for i in range(n_ep)]
        dsems = {}
        for q in self.dq:
            if any(t is not None for t in self.dma_last[q]):
                for i in range(self.n_dma_sems):
                    dsems[(q, i)] = st.enter_context(nc.semaphore(f"s_d{q}{i}"))

        def resolve(tok):
            if tok[0] == "e":
                idx = tok[2]
                ep = (idx - 1) // EPOCH
                return esems[tok[1]][ep], idx - ep * EPOCH, ("e", tok[1]), idx
            return dsems[tok[1]], tok[2], ("d", tok[1]), tok[2]

        final_deps = list(self.out_tokens)
        streams = self.streams

        def run(engname, eng):
            waited = {}
            for deps, fn, tok in streams[engname]:
                best = {}
                for d in deps:
                    if d[0] == "e" and d[1] == engname:
                        if engname in ("pe", "sync") or not same_engine_sync:
                            continue
                    sem, val, key, mono = resolve(d)
                    if waited.get(key, 0) >= mono:
                        continue
                    if key not in best or best[key][2] < mono:
                        best[key] = (sem, val, mono)
                for key, (sem, val, mono) in best.items():
                    eng.wait_ge(sem, val)
                    waited[key] = mono
                if fn is None:
                    continue
                inst = fn(eng)
                sem, val, key, mono = resolve(tok)
                inst.then_inc(sem, 16 if tok[0] == "d" else 1)
            if engname == "sync":
                best = {}
                for d in final_deps:
                    sem, val, key, mono = resolve(d)
                    if waited.get(key, 0) >= mono:
                        continue
                    if key not in best or best[key][2] < mono:
                        best[key] = (sem, val, mono)
                for key, (sem, val, mono) in best.items():
                    eng.wait_ge(sem, val)

        with nc.Block() as block:
            @block.sync
            def _(e):
                run("sync", e)

            @block.tensor
            def _(e):
                run("pe", e)

            @block.scalar
            def _(e):
                run("act", e)

            @block.vector
            def _(e):
                run("dve", e)

            @block.gpsimd
            def _(e):
                run("pool", e)
        st.close()


def _prod(xs):
    r = 1
    for x in xs:
        r *= x
    return r


class ABuf:
    def __init__(self, t, part_pages):
        self.t = t
        self.pp = part_pages

    @property
    def all(self):
        seen = []
        ids = set()
        for pl in self.pp:
            for r in pl:
                if id(r) not in ids:
                    ids.add(id(r)); seen.append(r)
        return seen

    def __getitem__(self, i):
        return list(self.pp[i])


class Arena:
    PAGE = 256

    def __init__(self, P, words):
        self.P = P
        self.t = P.stack.enter_context(P.nc.sbuf_tensor("arena", [128, words], F32))
        self.words = words
        self.pages = [Res() for _ in range((words + self.PAGE - 1) // self.PAGE)]
        self.top = 0
        self.peak = 0

    def mark(self):
        return self.top

    def release(self, m):
        self.top = m

    def alloc(self, shape, dtype, nparts=1):
        rows = shape[0]
        free = _prod(shape[1:])
        esz = 4 if dtype == F32 else 2
        words = (free * esz + 3) // 4
        words_al = ((words + self.PAGE - 1) // self.PAGE) * self.PAGE
        off = self.top
        assert off + words_al <= self.words, f"arena overflow {off}+{words_al}>{self.words}"
        self.top += words_al
        self.peak = max(self.peak, self.top)
        v = self.t[:rows, off:off + words]
        if dtype != F32:
            v = v.bitcast(dtype)[:, :free]
        if len(shape) > 2:
            names = " ".join(f"d{i}" for i in range(len(shape) - 1))
            kw = {f"d{i}": shape[i + 1] for i in range(len(shape) - 1)}
            v = v.rearrange(f"p ({names}) -> p {names}", **kw)
        pw = words / nparts
        pp = []
        for i in range(nparts):
            a = off + int(i * pw)
            b = off + max(int((i + 1) * pw) - 1, int(i * pw))
            pp.append(self.pages[a // self.PAGE: b // self.PAGE + 1])
        return ABuf(v, pp)


D = 2048
KC = 16
FF = 5632
F2 = 11264
NPC = 44
DEPTH = 4
SEQ = 2048
TP = 512
NS = 16
TS = 4
EPS = 1e-6
WSLOT = 8192
NWSLOT = 2


class Tile:
    pass


class Builder:
    def __init__(self, depth=DEPTH, tiles=None):
        self.depth = depth
        nc = bass.Bass("TRN2", target_bir_lowering=False)
        self.nc = nc
        P = Prog(nc, n_dma_sems=16)
        self.P = P
        self.dram = {}
        self.wq = "pool"
        di = self.din
        di("xTp", [D, SEQ]); di("xTs", [D, NS * TS])
        di("S_in", [2, NS, 16, 128, 128]); di("CA_in", [NS, 8, 128, 257]); di("m_in", [8, NS])
        di("conv_in", [4, F2, NS, 2])
        di("nrm", [128, 9, 16]); di("hlb", [128, 2, 16]); di("honorm", [128, 2, 16])
        di("mbif", [8, 2]); di("mhnorm", [128, 16]); di("mwif", [D, 16])
        di("gbu", [128, 16]); di("gbv", [128, D]); di("gvg", [128, D]); di("gvb", [128, D])
        di("gwsT", [128, 16, 128]); di("gwsTs", [64, 16, 64]); di("gbs", [1, 16, 128]); di("gbss", [1, 16, 64])
        di("cwv", [128, 4, 4, 88])
        di("maskp", [128, 128]); di("masks", [64, 64]); di("seq1h", [64, 16]); di("ident", [128, 128])
        di("ident8", [8, 8]); di("ones8", [8, 128]); di("sel", [8, 8, 128])
        for nm, shp in [("hgrn_w_q", [2, D, D]), ("hgrn_w_f", [2, D, D]), ("hgrn_w_i", [2, D, D]), ("hgrn_w_g", [2, D, D]),
                        ("hgrn_w_o", [2, D, D]), ("mlstm_w_q", [1, D, 1024]), ("mlstm_w_k", [1, D, 1024]),
                        ("mlstm_w_v", [1, D, D]), ("mlstm_w_og", [1, D, D]), ("mlstm_w_out", [1, D, D]),
                        ("gmlp_w_in", [1, D, 2 * D]), ("gmlp_w_out", [1, D, D]),
                        ("ffn_w_up", [4, D, F2]), ("ffn_w_down", [4, FF, D])]:
            di(nm, shp)
        do = self.dout
        do("yTp", [D, SEQ]); do("yTs", [D, NS * TS])
        do("S_p", [2, 16, 128, 128]); do("S_s", [2, NS, 16, 128, 128])
        do("CA_p", [8, 128, 257]); do("CA_s", [NS, 8, 128, 257])
        do("m_p", [8, 1]); do("m_s", [8, NS])
        do("v_s", [NS * TS, D])
        do("conv_p", [4, F2, 2]); do("conv_s", [4, F2, NS, 2])

        A = Arena(P, 53200)
        self.A = A
        self.PS = [P.psum(f"ps{i}", [128, 512], F32) for i in range(8)]
        self.psi = 0
        self.X = A.alloc([128, KC, TP], F32, nparts=KC)
        self.H = A.alloc([128, KC, TP], BF16, nparts=KC)
        self.WS = [A.alloc([128, WSLOT], BF16) for _ in range(NWSLOT)]
        self.wi = 0
        self.SP = [A.alloc([128, 16, 128], F32, nparts=16) for _ in range(2)]
        self.CAP = A.alloc([128, 8, 257], F32, nparts=8)
        self.CARRY = A.alloc([128, 4, 88, 2], F32, nparts=4)
        self.MLG = A.alloc([8, 4], F32)
        self.cst = {}
        self.load_consts()
        self.init_state()
        print("arena persistent words", A.top)
        P.barrier()
        if tiles is None:
            tiles = [("p", i) for i in range(SEQ // TP)] + [("s", 0)]
        for kind, i in tiles:
            self.run_tile(kind, i)
        P.emit()
        print("arena peak words", A.peak, "ops", dict(P.count))

    def din(self, name, shape):
        self.dram[name] = self.nc.dram_tensor(name, list(shape), F32, kind="ExternalInput").ap()

    def dout(self, name, shape):
        self.dram[name] = self.nc.dram_tensor(name, list(shape), F32, kind="ExternalOutput").ap()

    def ps(self):
        b = self.PS[self.psi]
        self.psi = (self.psi + 1) % 8
        return b

    def const(self, name, shape, dtype=F32, src=None):
        b = self.A.alloc(shape, dtype)
        src = self.dram[name] if src is None else src
        q = "sync" if dtype == F32 else "pool"
        self.P.dma(q, b.t, src, reads=[], writes=b.all)
        self.cst[name] = b
        return b

    def load_consts(self):
        c = self.const
        c("nrm", [128, 9, 16]); c("hlb", [128, 2, 16]); c("honorm", [128, 2, 16])
        c("mbif", [8, 2]); c("mhnorm", [128, 16])
        c("gbu", [128, 16]); c("cwv", [128, 4, 4, 88])
        c("maskp", [128, 128]); c("masks", [64, 64]); c("seq1h", [64, 16])
        c("identb", [128, 128], BF16, src=self.dram["ident"])
        c("ident8", [8, 8]); c("ones8", [8, 128]); c("sel", [8, 8, 128])
        P = self.P
        ones = self.A.alloc([128, 128], BF16)
        P.op("dve", lambda e: e.memset(ones.t, 1.0), [], ones.all)
        self.cst["onesb"] = ones
        onesf = self.A.alloc([128, 128], F32)
        P.op("dve", lambda e: e.memset(onesf.t, 1.0), [], onesf.all)
        self.cst["onesf"] = onesf
        self.epsb = self.A.alloc([128, 1], F32)
        P.op("dve", lambda e: e.memset(self.epsb.t, EPS), [], self.epsb.all)
        LB = self.A.alloc([128, 2, 16], F32)
        OML = self.A.alloc([128, 2, 16], F32)
        hlb = self.cst["hlb"]
        P.op("dve", lambda e: e.memset(LB.t, 0.0), [], LB.all)
        tmp = self.A.alloc([128, 16], F32)
        P.op("dve", lambda e: e.tensor_tensor(out=tmp.t, in0=hlb.t[:, 0, :], in1=hlb.t[:, 1, :], op=ALU.subtract), hlb.all, tmp.all)
        P.op("act", lambda e: e.activation(out=tmp.t, in_=tmp.t, func=AF.Exp), tmp.all, tmp.all)
        P.op("dve", lambda e: e.tensor_scalar(out=tmp.t, in0=tmp.t, scalar1=1.0, scalar2=None, op0=ALU.add), tmp.all, tmp.all)
        P.op("dve", lambda e: e.reciprocal(out=LB.t[:, 1, :], in_=tmp.t), tmp.all, LB.all)
        P.op("dve", lambda e: e.tensor_scalar(out=OML.t, in0=LB.t, scalar1=-1.0, scalar2=1.0, op0=ALU.mult, op1=ALU.add), LB.all, OML.all)
        self.cst["LB"] = LB; self.cst["OML"] = OML

    def init_state(self):
        P = self.P
        for b in self.SP + [self.CAP, self.CARRY, self.MLG]:
            P.op("dve", lambda e, b=b: e.memset(b.t, 0.0), [], b.all)

    def wload(self, src, shape):
        slot = self.WS[self.wi]
        self.wi = (self.wi + 1) % NWSLOT
        n = _prod(shape[1:])
        assert n <= WSLOT
        v = slot.t[:, :n]
        if len(shape) == 3:
            v = v.rearrange("p (a b) -> p a b", a=shape[1])
        self.P.dma(self.wq, v, src, reads=[], writes=slot.all)
        return ABuf(v, slot.pp)

    def wcols(self, name, idx, c0, n):
        return self.dram[name][idx].rearrange("(c p) f -> p c f", p=128)[:, :, c0:c0 + n]

    def act(self, out, in_, func, R, W, scale=1.0, bias=0.0):
        self.P.op("act", lambda e: e.activation(out=out, in_=in_, func=func, bias=bias, scale=scale), R, W)

    def tt(self, out, in0, in1, op, R, W, eng="dve"):
        self.P.op(eng, lambda e: e.tensor_tensor(out=out, in0=in0, in1=in1, op=op), R, W)

    def ts(self, out, in0, s1, s2, op0, op1, R, W, eng="dve"):
        if s2 is None:
            self.P.op(eng, lambda e: e.tensor_scalar(out=out, in0=in0, scalar1=s1, scalar2=None, op0=op0), R, W)
        else:
            self.P.op(eng, lambda e: e.tensor_scalar(out=out, in0=in0, scalar1=s1, scalar2=s2, op0=op0, op1=op1), R, W)

    def stt(self, out, in0, scalar, in1, op0, op1, R, W):
        self.P.op("dve", lambda e: e.scalar_tensor_tensor(out=out, in0=in0, scalar=scalar, in1=in1, op0=op0, op1=op1), R, W)

    def recip(self, out, in_, R, W):
        self.P.op("dve", lambda e: e.reciprocal(out=out, in_=in_), R, W)

    def cp(self, out, in_, R, W, eng="dve"):
        if eng == "act":
            self.P.op("act", lambda e: e.activation(out=out, in_=in_, func=AF.Copy), R, W)
        else:
            self.P.op(eng, lambda e: e.tensor_copy(out=out, in_=in_), R, W)

    def mm(self, out, lhsT, rhs, start, stop, R, W):
        self.P.op("pe", lambda e: e.matmul(out, lhsT=lhsT, rhs=rhs, start=start, stop=stop), R, W)

    def sigmoid_from(self, dst, src, R, W, scale=1.0):
        self.act(dst.t if hasattr(dst, "t") else dst, src, AF.Exp, R, W, scale=-scale)
        d = dst.t if hasattr(dst, "t") else dst
        self.ts(d, d, 1.0, None, ALU.add, None, W, W)
        self.recip(d, d, W, W)

    def rmsnorm_fm(self, src, nchunks, N, gcol_fn, dst, c0=0):
        A = self.A
        m = A.mark()
        sq = [A.alloc([128, TP], BF16) for _ in range(2)]
        rstd = A.alloc([128, TP], F32)
        ps = self.ps()
        ones = self.cst["onesb"]
        for c in range(nchunks):
            s = sq[c % 2]
            self.act(s.t[:, :N], src.t[:, c0 + c, :N], AF.Square, src[c0 + c], s.all)
            self.mm(ps.t[:, :N], ones.t, s.t[:, :N], c == 0, c == nchunks - 1, ones.all + s.all, ps.all)
        self.act(rstd.t[:, :N], ps.t[:, :N], AF.Ln, ps.all, rstd.all, scale=1.0 / (nchunks * 128), bias=self.epsb.t[:, 0:1])
        self.act(rstd.t[:, :N], rstd.t[:, :N], AF.Exp, rstd.all, rstd.all, scale=-0.5)
        for c in range(nchunks):
            self.stt(dst.t[:, c0 + c, :N], src.t[:, c0 + c, :N], gcol_fn(c0 + c), rstd.t[:, :N], ALU.mult, ALU.mult,
                     src[c0 + c] + rstd.all, dst[c0 + c])
        A.release(m)

    def ffn(self, tc, l):
        A, P, X, H = self.A, self.P, self.X, self.H
        N, NSEQ, T = tc.N, tc.NSEQ, tc.T
        nrm = self.cst["nrm"]; cwv = self.cst["cwv"]
        self.rmsnorm_fm(X, KC, N, lambda c: nrm.t[:, 4 + l, c:c + 1], H)
        m = A.mark()
        ACTB = A.alloc([128, NPC, N], BF16, nparts=NPC)
        UPX = [A.alloc([128, 2 + TP], F32) for _ in range(3)]
        YA = [A.alloc([128, TP], F32) for _ in range(4)]
        YG = [A.alloc([128, TP], F32) for _ in range(2)]
        if tc.kind == "s":
            CONVS = A.alloc([128, 88, NS, 2], F32, nparts=88)
            P.dma("sync", CONVS.t, self.dram["conv_in"][l].rearrange("(c p) s j -> p c s j", p=128), [], CONVS.all)
        ui = 0

        def v3(ap):
            return ap.rearrange("p (s t) -> p s t", s=NSEQ)

        def half(ps, chunk, ybuf):
            nonlocal ui
            upx = UPX[ui % 3]; ui += 1
            u3 = v3(upx.t[:, :NSEQ * (2 + T)])
            p3 = v3(ps.t[:, :N])
            y3 = v3(ybuf.t[:, :N])
            if tc.kind == "p":
                cv = self.CARRY.t[:, l, chunk:chunk + 1, :]
                cres = self.CARRY[l]
            else:
                cv = CONVS.t[:, chunk, :, :]
                cres = CONVS[chunk]
            cw = lambda j: cwv.t[:, l, j, chunk:chunk + 1]
            self.cp(u3[:, :, 0:2], cv, cres, upx.all)
            self.cp(u3[:, :, 2:2 + T], p3, ps.all, upx.all, eng="act")
            P.op("act", lambda e: e.activation(out=y3, in_=p3, func=AF.Identity, bias=cw(3), scale=cw(2)), ps.all + cwv.all, ybuf.all)
            self.stt(y3, u3[:, :, 1:1 + T], cw(1), y3, ALU.mult, ALU.add, upx.all + ybuf.all + cwv.all, ybuf.all)
            self.stt(y3, u3[:, :, 0:T], cw(0), y3, ALU.mult, ALU.add, upx.all + ybuf.all + cwv.all, ybuf.all)
            self.cp(cv, u3[:, :, T:T + 2], upx.all, cres)

        for g in range(NPC // 4):
            wa = self.wload(self.wcols("ffn_w_up", l, g * 512, 512), [128, KC, 512])
            pss = []
            for jj in range(4):
                ps = self.ps()
                for c in range(KC):
                    self.mm(ps.t[:, :N], wa.t[:, c, jj * 128:(jj + 1) * 128], H.t[:, c, :N], c == 0, c == KC - 1, wa.all + H[c], ps.all)
                half(ps, g * 4 + jj, YA[jj])
            wg = self.wload(self.wcols("ffn_w_up", l, FF + g * 512, 512), [128, KC, 512])
            for jj in range(4):
                ps = self.ps()
                for c in range(KC):
                    self.mm(ps.t[:, :N], wg.t[:, c, jj * 128:(jj + 1) * 128], H.t[:, c, :N], c == 0, c == KC - 1, wg.all + H[c], ps.all)
                yg = YG[jj % 2]
                half(ps, NPC + g * 4 + jj, yg)
                self.act(yg.t[:, :N], yg.t[:, :N], AF.Silu, yg.all, yg.all)
                i = g * 4 + jj
                self.tt(ACTB.t[:, i, :N], YA[jj].t[:, :N], yg.t[:, :N], ALU.mult, YA[jj].all + yg.all, ACTB[i])
        wdv = self.dram["ffn_w_down"][l].rearrange("(c p) f -> p c f", p=128)
        for dc in range(KC):
            wd = self.wload(wdv[:, :, dc * 128:(dc + 1) * 128], [128, NPC, 128])
            ps = self.ps()
            for c in range(NPC):
                self.mm(ps.t[:, :N], wd.t[:, c, :], ACTB.t[:, c, :N], c == 0, c == NPC - 1, wd.all + ACTB[c], ps.all)
            self.tt(X.t[:, dc, :N], X.t[:, dc, :N], ps.t[:, :N], ALU.add, X[dc] + ps.all, X[dc])
        if tc.kind == "p" and tc.last:
            P.dma("sync", self.dram["conv_p"][l].rearrange("(c p) j -> p c j", p=128), self.CARRY.t[:, l], self.CARRY[l], [], is_output=True)
        if tc.kind == "s":
            P.dma("sync", self.dram["conv_s"][l].rearrange("(c p) s j -> p c s j", p=128), CONVS.t, CONVS.all, [], is_output=True)
        A.release(m)

    def run_tile(self, kind, i):
        P, X = self.P, self.X
        tc = Tile()
        tc.kind = kind
        if kind == "p":
            tc.N, tc.NSEQ, tc.T, tc.L, tc.NCK = TP, 1, TP, 128, TP // 128
            tc.first = i == 0
            tc.last = i == SEQ // TP - 1
            tc.mask = self.cst["maskp"]
            xsrc = self.dram["xTp"].rearrange("(c p) t -> p c t", p=128)[:, :, i * TP:(i + 1) * TP]
            ydst = self.dram["yTp"].rearrange("(c p) t -> p c t", p=128)[:, :, i * TP:(i + 1) * TP]
        else:
            tc.N, tc.NSEQ, tc.T, tc.L, tc.NCK = NS * TS, NS, TS, NS * TS, 1
            tc.first = tc.last = True
            tc.mask = self.cst["masks"]
            xsrc = self.dram["xTs"].rearrange("(c p) t -> p c t", p=128)
            ydst = self.dram["yTs"].rearrange("(c p) t -> p c t", p=128)
        N = tc.N
        P.dma("sync", X.t[:, :, :N], xsrc, [], X.all)
        for l in range(self.depth):
            kindl = l % 3
            if kindl == 0:
                self.hgrn(tc, l, l // 3)
            elif kindl == 1:
                self.mlstm(tc, l)
            else:
                self.gmlp(tc, l)
            self.ffn(tc, l)
        A = self.A
        m = A.mark()
        if self.depth == DEPTH:
            Y = A.alloc([128, KC, N], F32, nparts=KC)
            nrm = self.cst["nrm"]
            self.rmsnorm_fm(X, KC, N, lambda c: nrm.t[:, 8, c:c + 1], Y)
            P.dma("sync", ydst, Y.t, Y.all, [], is_output=True)
        else:
            P.dma("sync", ydst, X.t[:, :, :N], X.all, [], is_output=True)
        A.release(m)

    def hgrn(self, tc, l, j):
        A, P, X, H = self.A, self.P, self.X, self.H
        N, NSEQ, T, L, NCK = tc.N, tc.NSEQ, tc.T, tc.L, tc.NCK
        nrm = self.cst["nrm"]; LB = self.cst["LB"]; OML = self.cst["OML"]
        identb = self.cst["identb"]; mask = tc.mask; seq1h = self.cst["seq1h"]
        samp = tc.kind == "s"
        self.rmsnorm_fm(X, KC, N, lambda c: nrm.t[:, l, c:c + 1], H)
        m0 = A.mark()
        OT = A.alloc([128, KC, N], F32, nparts=KC)
        for g in range(4):
            m1 = A.mark()
            BT = A.alloc([128, 4, N], F32, nparts=4)
            EQ = BT
            QT = A.alloc([128, 4, N], BF16, nparts=4)
            KT = A.alloc([128, 4, N], BF16, nparts=4)
            V = A.alloc([128, NCK, 512], BF16, nparts=NCK)
            KTOK = A.alloc([128, NCK, 512], BF16, nparts=NCK)
            TMP = [A.alloc([128, TP], F32) for _ in range(4)]
            nsc = NCK if not samp else NS
            EMID = A.alloc([128, 4, nsc], F32)
            EBL = A.alloc([128, 4, nsc], F32)
            EBLM = A.alloc([128, 4, nsc], F32)
            D1 = A.alloc([128, 4, nsc], F32)
            ti = 0
            wf = self.wload(self.wcols("hgrn_w_f", j, g * 512, 512), [128, KC, 512])
            for hh in range(4):
                head = g * 4 + hh
                ps = self.ps()
                for c in range(KC):
                    self.mm(ps.t[:, :N], wf.t[:, c, hh * 128:(hh + 1) * 128], H.t[:, c, :N], c == 0, c == KC - 1, wf.all + H[c], ps.all)
                t1 = TMP[ti % 4]; ti += 1
                t2 = TMP[ti % 4]; ti += 1
                self.sigmoid_from(t1.t[:, :N], ps.t[:, :N], ps.all, t1.all)
                self.ts(t2.t[:, :N], t1.t[:, :N], OML.t[:, j, head:head + 1], LB.t[:, j, head:head + 1], ALU.mult, ALU.add, t1.all + OML.all + LB.all, t2.all)
                self.act(t1.t[:, :N], t2.t[:, :N], AF.Ln, t2.all, t1.all)
                self.ts(t2.t[:, :N], t2.t[:, :N], -1.0, 1.0, ALU.mult, ALU.add, t2.all, t2.all)
                if not samp:
                    for ck in range(NCK):
                        cs = slice(ck * L, (ck + 1) * L)
                        P.op("dve", lambda e, cs=cs, t1=t1, hh=hh: e.tensor_tensor_scan(out=BT.t[:, hh, cs], data0=self.cst["onesf"].t[:, :L], data1=t1.t[:, cs], initial=0.0, op0=ALU.mult, op1=ALU.add),
                             t1.all + self.cst["onesf"].all, BT[hh])
                else:
                    b3 = BT.t[:, hh, :].rearrange("p (s t) -> p s t", t=T)
                    l3 = t1.t[:, :N].rearrange("p (s t) -> p s t", t=T)
                    self.cp(b3[:, :, 0:1], l3[:, :, 0:1], t1.all, BT[hh])
                    for tt_ in range(1, T):
                        self.tt(b3[:, :, tt_:tt_ + 1], b3[:, :, tt_ - 1:tt_], l3[:, :, tt_:tt_ + 1], ALU.add, t1.all + BT[hh], BT[hh])
                t3 = TMP[ti % 4]; ti += 1
                if not samp:
                    bv = BT.t[:, hh, :].rearrange("p (c t) -> p c t", t=L)
                    refv = bv[:, :, L // 2 - 1:L // 2]
                    blv = bv[:, :, L - 1:L]
                    self.cp(EMID.t[:, hh, :].unsqueeze(2), refv, BT[hh], EMID.all)
                    self.tt(D1.t[:, hh, :].unsqueeze(2), blv, refv, ALU.subtract, BT[hh], D1.all)
                    self.act(EBLM.t[:, hh, :], D1.t[:, hh, :], AF.Exp, D1.all, EBLM.all)
                    self.act(EBL.t[:, hh, :].unsqueeze(2), blv, AF.Exp, BT[hh], EBL.all)
                    self.tt(t3.t[:, :N].rearrange("p (c t) -> p c t", t=L), bv, refv.to_broadcast([128, NCK, L]), ALU.subtract, BT[hh], t3.all)
                    self.act(EMID.t[:, hh, :], EMID.t[:, hh, :], AF.Exp, EMID.all, EMID.all)
                    e2 = t3
                else:
                    b3 = BT.t[:, hh, :].rearrange("p (s t) -> p s t", t=T)
                    self.act(EBL.t[:, hh, :].unsqueeze(2), b3[:, :, T - 1:T], AF.Exp, BT[hh], EBL.all)
                    e2 = None
                e2ap = e2.t[:, :N] if e2 is not None else BT.t[:, hh, :]
                e2res = e2.all if e2 is not None else BT[hh]
                self.act(t1.t[:, :N], e2ap, AF.Exp, e2res, t1.all, scale=-1.0)
                self.tt(KT.t[:, hh, :], t2.t[:, :N], t1.t[:, :N], ALU.mult, t1.all + t2.all, KT[hh])
                self.act(EQ.t[:, hh, :], e2ap, AF.Exp, e2res, EQ[hh])
            wq_ = self.wload(self.wcols("hgrn_w_q", j, g * 512, 512), [128, KC, 512])
            for hh in range(4):
                ps = self.ps()
                for c in range(KC):
                    self.mm(ps.t[:, :N], wq_.t[:, c, hh * 128:(hh + 1) * 128], H.t[:, c, :N], c == 0, c == KC - 1, wq_.all + H[c], ps.all)
                t1 = TMP[ti % 4]; ti += 1
                self.sigmoid_from(t1.t[:, :N], ps.t[:, :N], ps.all, t1.all)
                self.tt(t1.t[:, :N], ps.t[:, :N], t1.t[:, :N], ALU.mult, ps.all + t1.all, t1.all)
                self.tt(QT.t[:, hh, :], t1.t[:, :N], EQ.t[:, hh, :], ALU.mult, t1.all + EQ[hh], QT[hh])
            wi_ = self.wload(self.wcols("hgrn_w_i", j, g * 512, 512), [128, KC, 512])
            for ck in range(NCK):
                ps = self.ps()
                for c in range(KC):
                    self.mm(ps.t[:L, :512], H.t[:, c, ck * L:(ck + 1) * L], wi_.t[:, c, :], c == 0, c == KC - 1, wi_.all + H[c], ps.all)
                self.cp(V.t[:L, ck, :], ps.t[:L, :512], ps.all, V[ck], eng="act")
            for hh in range(4):
                for ck in range(NCK):
                    ps = self.ps()
                    pb = ps.t[:, :64].bitcast(BF16)
                    P.op("pe", lambda e, pb=pb, hh=hh, ck=ck: e.transpose(out=pb[:L, :128], in_=KT.t[:, hh, ck * L:(ck + 1) * L], identity=identb.t),
                         KT[hh] + identb.all, ps.all)
                    self.cp(KTOK.t[:L, ck, hh * 128:(hh + 1) * 128], pb[:L, :128], ps.all, KTOK[ck])
            for hh in range(4):
                head = g * 4 + hh
                m2 = A.mark()
                if samp:
                    SS = A.alloc([128, NS, 128], F32)
                    P.dma("sync", SS.t, self.dram["S_in"][j, :, head].rearrange("s k v -> k s v"), [], SS.all)
                    ST = A.alloc([128, NS, 128], BF16)
                    self.cp(ST.t, SS.t, SS.all, ST.all, eng="act")
                    KM = A.alloc([64, NS, 128], BF16)
                    self.tt(KM.t, KTOK.t[:64, 0, hh * 128:(hh + 1) * 128].unsqueeze(1).to_broadcast([64, NS, 128]),
                            seq1h.t.unsqueeze(2).to_broadcast([64, NS, 128]), ALU.mult, KTOK[0] + seq1h.all, KM.all)
                else:
                    ST = A.alloc([128, 128], BF16)
                AT = A.alloc([128, 128], BF16)
                if not samp:
                    P.op("dve", lambda e, AT=AT: e.memset(AT.t[64:128, 0:64], 0.0), [], AT.all)
                for ck in range(NCK):
                    cs = slice(ck * L, (ck + 1) * L)
                    vh = V.t[:L, ck, hh * 128:(hh + 1) * 128]
                    psA = self.ps()
                    if samp:
                        self.mm(psA.t[:L, :L], KT.t[:, hh, cs], QT.t[:, hh, cs], True, True, KT[hh] + QT[hh], psA.all)
                        self.tt(AT.t[:L, :L], psA.t[:L, :L], mask.t[:L, :L], ALU.mult, psA.all + mask.all, AT.all)
                    else:
                        hf = L // 2
                        c1 = slice(ck * L, ck * L + hf); c2 = slice(ck * L + hf, (ck + 1) * L)
                        self.mm(psA.t[:L, hf:L], KT.t[:, hh, cs], QT.t[:, hh, c2], True, True, KT[hh] + QT[hh], psA.all)
                        self.mm(psA.t[:hf, 0:hf], KT.t[:, hh, c1], QT.t[:, hh, c1], True, True, KT[hh] + QT[hh], psA.all)
                        self.tt(AT.t[:L, hf:L], psA.t[:L, hf:L], mask.t[:L, hf:L], ALU.mult, psA.all + mask.all, AT.all)
                        self.tt(AT.t[:hf, 0:hf], psA.t[:hf, 0:hf], mask.t[:hf, 0:hf], ALU.mult, psA.all + mask.all, AT.all)
                    psO = self.ps()
                    self.mm(psO.t[:, :L], vh, AT.t[:L, :L], True, False, V[ck] + AT.all, psO.all)
                    if not samp:
                        S = self.SP[j]
                        P.op("act", lambda e, ck=ck, hh=hh, head=head, S=S: e.activation(out=ST.t, in_=S.t[:, head, :], func=AF.Copy, scale=EMID.t[:, hh, ck:ck + 1]),
                             S[head] + EMID.all, ST.all)
                        self.mm(psO.t[:, :L], ST.t, QT.t[:, hh, cs], False, True, ST.all + QT[hh], psO.all)
                    else:
                        for s in range(NS):
                            self.mm(psO.t[:, s * T:(s + 1) * T], ST.t[:, s, :], QT.t[:, hh, s * T:(s + 1) * T], False, s == NS - 1, ST.all + QT[hh], psO.all)
                    self.cp(OT.t[:, head, cs], psO.t[:, :L], psO.all, OT[head], eng="act")
                    if not samp:
                        psS = self.ps()
                        self.mm(psS.t[:, :128], KTOK.t[:L, ck, hh * 128:(hh + 1) * 128], vh, True, True, KTOK[ck] + V[ck], psS.all)
                        self.ts(S.t[:, head, :], S.t[:, head, :], EBL.t[:, hh, ck:ck + 1], None, ALU.mult, None, S[head] + EBL.all, S[head])
                        self.stt(S.t[:, head, :], psS.t[:, :128], EBLM.t[:, hh, ck:ck + 1], S.t[:, head, :], ALU.mult, ALU.add, psS.all + EBLM.all + S[head], S[head])
                    else:
                        for b4 in range(NS // 4):
                            psS = self.ps()
                            for s4 in range(4):
                                s = b4 * 4 + s4
                                self.mm(psS.t[:, s4 * 128:(s4 + 1) * 128], KM.t[:, s, :], vh, True, True, KM.all + V[ck], psS.all)
                            sv = SS.t[:, b4 * 4:(b4 + 1) * 4, :]
                            self.tt(sv, sv, psS.t[:, :512].rearrange("p (s v) -> p s v", s=4), ALU.add, SS.all + psS.all, SS.all)
                            self.tt(sv, sv, EBL.t[:, hh, b4 * 4:(b4 + 1) * 4].unsqueeze(2).to_broadcast([128, 4, 128]), ALU.mult, SS.all + EBL.all, SS.all)
                        P.dma("sync", self.dram["S_s"][j, :, head].rearrange("s k v -> k s v"), SS.t, SS.all, [], is_output=True)
                A.release(m2)
            A.release(m1)
        if not samp and tc.last:
            P.dma("sync", self.dram["S_p"][j].rearrange("h k v -> k h v"), self.SP[j].t, self.SP[j].all, [], is_output=True)
        honorm = self.cst["honorm"]
        ON = OT
        self.rmsnorm_fm(OT, KC, N, lambda c: honorm.t[:, j, c:c + 1], ON)
        OG = A.alloc([128, KC, N], BF16, nparts=KC)
        TMP = [A.alloc([128, TP], F32) for _ in range(2)]
        for g in range(4):
            wg_ = self.wload(self.wcols("hgrn_w_g", j, g * 512, 512), [128, KC, 512])
            for hh in range(4):
                cch = g * 4 + hh
                ps = self.ps()
                for c in range(KC):
                    self.mm(ps.t[:, :N], wg_.t[:, c, hh * 128:(hh + 1) * 128], H.t[:, c, :N], c == 0, c == KC - 1, wg_.all + H[c], ps.all)
                t1 = TMP[cch % 2]
                self.sigmoid_from(t1.t[:, :N], ps.t[:, :N], ps.all, t1.all)
                self.tt(OG.t[:, cch, :], ON.t[:, cch, :], t1.t[:, :N], ALU.mult, ON[cch] + t1.all, OG[cch])
        self.out_proj("hgrn_w_o", j, OG, N)
        A.release(m0)

    def out_proj(self, wname, idx, OG, N):
        X = self.X
        for g in range(4):
            wo = self.wload(self.wcols(wname, idx, g * 512, 512), [128, KC, 512])
            for hh in range(4):
                dc = g * 4 + hh
                ps = self.ps()
                for c in range(KC):
                    self.mm(ps.t[:, :N], wo.t[:, c, hh * 128:(hh + 1) * 128], OG.t[:, c, :N], c == 0, c == KC - 1, wo.all + OG[c], ps.all)
                self.tt(X.t[:, dc, :N], X.t[:, dc, :N], ps.t[:, :N], ALU.add, X[dc] + ps.all, X[dc])

    def mlstm(self, tc, l):
        A, P, X, H = self.A, self.P, self.X, self.H
        N, NSEQ, T, L, NCK = tc.N, tc.NSEQ, tc.T, tc.L, tc.NCK
        samp = tc.kind == "s"
        nrm = self.cst["nrm"]; mask = tc.mask; seq1h = self.cst["seq1h"]; identb = self.cst["identb"]
        sel = self.cst["sel"]; ones8 = self.cst["ones8"]; ident8 = self.cst["ident8"]; onesb = self.cst["onesb"]
        onesf = self.cst["onesf"]; mbif = self.cst["mbif"]; mhnorm = self.cst["mhnorm"]
        self.rmsnorm_fm(X, KC, N, lambda c: nrm.t[:, l, c:c + 1], H)
        m0 = A.mark()
        OUT = A.alloc([128, KC, N], F32, nparts=KC)
        nsc = NS if samp else NCK
        mg = A.mark()
        WIF = A.alloc([128, KC, 16], BF16)
        P.dma(self.wq, WIF.t, self.dram["mwif"].rearrange("(c p) f -> p c f", p=128), [], WIF.all)
        g8 = lambda: A.alloc([8, TP], F32)
        IG, FG, Bc, Aa, Gc, AE = g8(), g8(), g8(), g8(), g8(), g8()
        BE = IG; t8 = AE
        NB2 = A.alloc([8, 2], F32)
        self.ts(NB2.t, mbif.t, -2.0 / 15.0, None, ALU.mult, None, mbif.all, NB2.all)
        for gi, dst in ((0, IG), (1, FG)):
            ps = self.ps()
            for c in range(KC):
                self.mm(ps.t[:8, :N], WIF.t[:, c, gi * 8:(gi + 1) * 8], H.t[:, c, :N], c == 0, c == KC - 1, WIF.all + H[c], ps.all)
            self.act(dst.t[:, :N], ps.t[:8, :N], AF.Exp, ps.all + NB2.all, dst.all, scale=-2.0 / 15.0, bias=NB2.t[:, gi:gi + 1])
            self.ts(dst.t[:, :N], dst.t[:, :N], 1.0, None, ALU.add, None, dst.all, dst.all)
            self.recip(dst.t[:, :N], dst.t[:, :N], dst.all, dst.all)
            self.ts(dst.t[:, :N], dst.t[:, :N], 30.0, -15.0, ALU.mult, ALU.add, dst.all, dst.all)
        self.act(t8.t[:, :N], FG.t[:, :N], AF.Exp, FG.all, t8.all, scale=-1.0)
        self.ts(t8.t[:, :N], t8.t[:, :N], 1.0, None, ALU.add, None, t8.all, t8.all)
        self.act(t8.t[:, :N], t8.t[:, :N], AF.Ln, t8.all, t8.all)
        self.ts(FG.t[:, :N], t8.t[:, :N], -1.0, None, ALU.mult, None, t8.all, FG.all)
        GP = A.alloc([8, nsc], F32); GL = A.alloc([8, nsc], F32); CS = A.alloc([8, nsc], F32)
        MLG = self.MLG
        if not samp:
            for ck in range(NCK):
                cs = slice(ck * L, (ck + 1) * L)
                ini = MLG.t[:, 0:1] if ck == 0 else Bc.t[:, ck * L - 1:ck * L]
                P.op("dve", lambda e, cs=cs, ini=ini: e.tensor_tensor_scan(out=Bc.t[:, cs], data0=onesf.t[:8, :L], data1=FG.t[:, cs], initial=ini, op0=ALU.mult, op1=ALU.add),
                     FG.all + MLG.all + onesf.all + Bc.all, Bc.all)
            self.tt(Aa.t[:, :N], IG.t[:, :N], Bc.t[:, :N], ALU.subtract, IG.all + Bc.all, Aa.all)
            P.op("dve", lambda e: e.tensor_tensor_scan(out=Gc.t[:, :N], data0=Aa.t[:, :N], data1=Aa.t[:, :N], initial=MLG.t[:, 1:2], op0=ALU.max, op1=ALU.max),
                 Aa.all + MLG.all, Gc.all)
            gv = Gc.t[:, :N].rearrange("p (c t) -> p c t", t=L)
            self.cp(GP.t[:, 0:1], MLG.t[:, 1:2], MLG.all, GP.all)
            if NCK > 1:
                self.cp(GP.t[:, 1:NCK].unsqueeze(2), gv[:, 0:NCK - 1, L - 1:L], Gc.all, GP.all)
            self.cp(GL.t[:, :].unsqueeze(2), gv[:, :, L - 1:L], Gc.all, GL.all)
            for ck in range(NCK):
                cs = slice(ck * L, (ck + 1) * L)
                self.ts(AE.t[:, cs], Aa.t[:, cs], GP.t[:, ck:ck + 1], None, ALU.subtract, None, Aa.all + GP.all, AE.all)
                self.ts(BE.t[:, cs], Bc.t[:, cs], GP.t[:, ck:ck + 1], -1.0, ALU.add, ALU.mult, Bc.all + GP.all, BE.all)
            if tc.last:
                MO = A.alloc([8, 1], F32)
                self.tt(MO.t, Bc.t[:, N - 1:N], Gc.t[:, N - 1:N], ALU.add, Bc.all + Gc.all, MO.all)
                P.dma("sync", self.dram["m_p"], MO.t, MO.all, [], is_output=True)
            self.cp(MLG.t[:, 0:1], Bc.t[:, N - 1:N], Bc.all + GP.all, MLG.all)
            self.cp(MLG.t[:, 1:2], Gc.t[:, N - 1:N], Gc.all + GP.all, MLG.all)
        else:
            MIN = A.alloc([8, NS], F32)
            P.dma("sync", MIN.t, self.dram["m_in"], [], MIN.all)
            v3 = lambda b: b.t[:, :N].rearrange("p (s t) -> p s t", t=T)
            b3, f3, a3, g3, i3 = v3(Bc), v3(FG), v3(Aa), v3(Gc), v3(IG)
            self.cp(b3[:, :, 0:1], f3[:, :, 0:1], FG.all, Bc.all)
            for t_ in range(1, T):
                self.tt(b3[:, :, t_:t_ + 1], b3[:, :, t_ - 1:t_], f3[:, :, t_:t_ + 1], ALU.add, FG.all + Bc.all, Bc.all)
            self.tt(Aa.t[:, :N], IG.t[:, :N], Bc.t[:, :N], ALU.subtract, IG.all + Bc.all, Aa.all)
            self.tt(g3[:, :, 0:1], a3[:, :, 0:1], MIN.t.unsqueeze(2), ALU.max, Aa.all + MIN.all, Gc.all)
            for t_ in range(1, T):
                self.tt(g3[:, :, t_:t_ + 1], g3[:, :, t_ - 1:t_], a3[:, :, t_:t_ + 1], ALU.max, Aa.all + Gc.all, Gc.all)
            self.cp(GP.t, MIN.t, MIN.all, GP.all)
            self.cp(GL.t.unsqueeze(2), g3[:, :, T - 1:T], Gc.all, GL.all)
            gpb = GP.t.unsqueeze(2).to_broadcast([8, NS, T])
            self.tt(v3(AE), a3, gpb, ALU.subtract, Aa.all + GP.all, AE.all)
            self.tt(v3(BE), b3, gpb, ALU.add, Bc.all + GP.all, BE.all)
            self.ts(BE.t[:, :N], BE.t[:, :N], -1.0, None, ALU.mult, None, BE.all, BE.all)
            MO = A.alloc([8, NS], F32)
            self.tt(MO.t.unsqueeze(2), b3[:, :, T - 1:T], g3[:, :, T - 1:T], ALU.add, Bc.all + Gc.all, MO.all)
            P.dma("sync", self.dram["m_s"], MO.t, MO.all, [], is_output=True)
        self.tt(CS.t, GP.t, GL.t, ALU.subtract, GP.all + GL.all, CS.all)
        self.act(CS.t, CS.t, AF.Exp, CS.all, CS.all)
        DG = A.alloc([8, 8, nsc], F32)
        self.tt(DG.t, CS.t.unsqueeze(1).to_broadcast([8, 8, nsc]), ident8.t.unsqueeze(2).to_broadcast([8, 8, nsc]), ALU.mult, CS.all + ident8.all, DG.all)
        psb = self.ps()
        self.mm(psb.t[:, :8 * nsc], ones8.t, DG.t.rearrange("p a b -> p (a b)"), True, True, ones8.all + DG.all, psb.all)
        CSB = A.alloc([128, 8, nsc], F32)
        self.cp(CSB.t.rearrange("p a b -> p (a b)"), psb.t[:, :8 * nsc], psb.all, CSB.all)
        KSC = float(128 ** -0.5)
        wkv = self.dram["mlstm_w_k"][0].rearrange("(c p) f -> p c f", p=128)
        wqv = self.dram["mlstm_w_q"][0].rearrange("(c p) f -> p c f", p=128)
        for hp in range(4):
            m1 = A.mark()
            KT = A.alloc([128, 2, N], BF16); QT = A.alloc([128, 2, N], BF16)
            KTOK = A.alloc([128, NCK, 256], BF16, nparts=NCK)
            VA = A.alloc([128, NCK, 2, 257], BF16, nparts=NCK)
            P.op("dve", lambda e, VA=VA: e.memset(VA.t[:, :, :, 256:257], 1.0), [], VA.all)
            EAb = A.alloc([128, TP], F32); EBb = A.alloc([128, 128], F32)
            wk = self.wload(wkv[:, :, hp * 256:(hp + 1) * 256], [128, KC, 256])
            for hh in range(2):
                h = hp * 2 + hh
                ps = self.ps()
                for c in range(KC):
                    self.mm(ps.t[:, :N], wk.t[:, c, hh * 128:(hh + 1) * 128], H.t[:, c, :N], c == 0, c == KC - 1, wk.all + H[c], ps.all)
                pe_ = self.ps()
                self.mm(pe_.t[:, :N], sel.t[:, h, :], AE.t[:, :N], True, True, sel.all + AE.all, pe_.all)
                self.act(EAb.t[:, :N], pe_.t[:, :N], AF.Exp, pe_.all, EAb.all)
                self.stt(KT.t[:, hh, :], ps.t[:, :N], KSC, EAb.t[:, :N], ALU.mult, ALU.mult, ps.all + EAb.all, KT.all)
                for ck in range(NCK):
                    pt = self.ps()
                    pb = pt.t[:, :64].bitcast(BF16)
                    P.op("pe", lambda e, pb=pb, hh=hh, ck=ck, KT=KT: e.transpose(out=pb[:L, :128], in_=KT.t[:, hh, ck * L:(ck + 1) * L], identity=identb.t),
                         KT.all + identb.all, pt.all)
                    self.cp(KTOK.t[:L, ck, hh * 128:(hh + 1) * 128], pb[:L, :128], pt.all, KTOK[ck])
            wq_ = self.wload(wqv[:, :, hp * 256:(hp + 1) * 256], [128, KC, 256])
            for hh in range(2):
                ps = self.ps()
                for c in range(KC):
                    self.mm(ps.t[:, :N], wq_.t[:, c, hh * 128:(hh + 1) * 128], H.t[:, c, :N], c == 0, c == KC - 1, wq_.all + H[c], ps.all)
                self.cp(QT.t[:, hh, :], ps.t[:, :N], ps.all, QT.all, eng="act")
            wv = self.wload(self.wcols("mlstm_w_v", 0, hp * 512, 512), [128, KC, 512])
            for ck in range(NCK):
                ps = self.ps()
                for c in range(KC):
                    self.mm(ps.t[:L, :512], H.t[:, c, ck * L:(ck + 1) * L], wv.t[:, c, :], c == 0, c == KC - 1, wv.all + H[c], ps.all)
                self.cp(VA.t[:L, ck, :, 0:256], ps.t[:L, :512].rearrange("p (a b) -> p a b", a=2), ps.all, VA[ck], eng="act")
            for hh in range(2):
                h = hp * 2 + hh
                m2 = A.mark()
                WT = A.alloc([128, 128], BF16)
                DEN = A.alloc([128, 128], F32)
                if samp:
                    CAS = A.alloc([128, NS, 257], F32)
                    P.dma("sync", CAS.t, self.dram["CA_in"][:, h].rearrange("s k v -> k s v"), [], CAS.all)
                    CAb = A.alloc([128, NS, 257], BF16)
                    self.cp(CAb.t, CAS.t, CAS.all, CAb.all, eng="act")
                    NB = A.alloc([128, NS, 128], BF16)
                    self.cp(NB.t, CAS.t[:, :, 256:257].to_broadcast([128, NS, 128]), CAS.all, NB.all)
                    KM = A.alloc([64, NS, 128], BF16)
                    self.tt(KM.t, KTOK.t[:64, 0, hh * 128:(hh + 1) * 128].unsqueeze(1).to_broadcast([64, NS, 128]),
                            seq1h.t.unsqueeze(2).to_broadcast([64, NS, 128]), ALU.mult, KTOK[0] + seq1h.all, KM.all)
                else:
                    CAb = A.alloc([128, 257], BF16)
                    NB = A.alloc([128, 128], BF16)
                for ck in range(NCK):
                    cs = slice(ck * L, (ck + 1) * L)
                    psP = self.ps()
                    self.mm(psP.t[:L, :L], KT.t[:, hh, cs], QT.t[:, hh, cs], True, True, KT.all + QT.all, psP.all)
                    self.tt(WT.t[:L, :L], psP.t[:L, :L], mask.t[:L, :L], ALU.mult, psP.all + mask.all, WT.all)
                    if not samp:
                        CA = self.CAP
                        self.cp(CAb.t, CA.t[:, h, :], CA[h], CAb.all, eng="act")
                        self.cp(NB.t, CA.t[:, h, 256:257].to_broadcast([128, 128]), CA[h], NB.all)
                    psD = self.ps()
                    self.mm(psD.t[:, :L], onesb.t[:L, :], WT.t[:L, :L], True, False, onesb.all + WT.all, psD.all)
                    if not samp:
                        self.mm(psD.t[:, :L], NB.t, QT.t[:, hh, cs], False, True, NB.all + QT.all, psD.all)
                    else:
                        for s in range(NS):
                            self.mm(psD.t[:, s * T:(s + 1) * T], NB.t[:, s, :], QT.t[:, hh, s * T:(s + 1) * T], False, s == NS - 1, NB.all + QT.all, psD.all)
                    pe_ = self.ps()
                    self.mm(pe_.t[:, :L], sel.t[:, h, :], BE.t[:, cs], True, True, sel.all + BE.all, pe_.all)
                    self.act(EBb.t[:, :L], pe_.t[:, :L], AF.Exp, pe_.all, EBb.all)
                    self.act(DEN.t[:, :L], psD.t[:, :L], AF.Abs, psD.all, DEN.all)
                    self.tt(DEN.t[:, :L], DEN.t[:, :L], EBb.t[:, :L], ALU.max, DEN.all + EBb.all, DEN.all)
                    self.recip(DEN.t[:, :L], DEN.t[:, :L], DEN.all, DEN.all)
                    for half in range(2):
                        psN = self.ps()
                        self.mm(psN.t[:, :L], VA.t[:L, ck, hh, half * 128:(half + 1) * 128], WT.t[:L, :L], True, False, VA[ck] + WT.all, psN.all)
                        if not samp:
                            self.mm(psN.t[:, :L], CAb.t[:, half * 128:(half + 1) * 128], QT.t[:, hh, cs], False, True, CAb.all + QT.all, psN.all)
                        else:
                            for s in range(NS):
                                self.mm(psN.t[:, s * T:(s + 1) * T], CAb.t[:, s, half * 128:(half + 1) * 128], QT.t[:, hh, s * T:(s + 1) * T], False, s == NS - 1, CAb.all + QT.all, psN.all)
                        oc = 2 * h + half
                        self.tt(OUT.t[:, oc, cs], psN.t[:, :L], DEN.t[:, :L], ALU.mult, psN.all + DEN.all, OUT[oc])
                    if not samp:
                        psC = self.ps()
                        self.mm(psC.t[:, :257], KTOK.t[:L, ck, hh * 128:(hh + 1) * 128], VA.t[:L, ck, hh, :], True, True, KTOK[ck] + VA[ck], psC.all)
                        self.tt(CA.t[:, h, :], CA.t[:, h, :], psC.t[:, :257], ALU.add, CA[h] + psC.all, CA[h])
                        self.ts(CA.t[:, h, :], CA.t[:, h, :], CSB.t[:, h, ck:ck + 1], None, ALU.mult, None, CA[h] + CSB.all, CA[h])
                    else:
                        for s in range(NS):
                            psC = self.ps()
                            self.mm(psC.t[:, :257], KM.t[:, s, :], VA.t[:64, 0, hh, :], True, True, KM.all + VA[0], psC.all)
                            self.tt(CAS.t[:, s, :], CAS.t[:, s, :], psC.t[:, :257], ALU.add, CAS.all + psC.all, CAS.all)
                            self.ts(CAS.t[:, s, :], CAS.t[:, s, :], CSB.t[:, h, s:s + 1], None, ALU.mult, None, CAS.all + CSB.all, CAS.all)
                        P.dma("sync", self.dram["CA_s"][:, h].rearrange("s k v -> k s v"), CAS.t, CAS.all, [], is_output=True)
                A.release(m2)
            A.release(m1)
        if not samp and tc.last:
            P.dma("sync", self.dram["CA_p"].rearrange("h k v -> k h v"), self.CAP.t, self.CAP.all, [], is_output=True)
        A.release(mg)
        for h in range(8):
            self.rmsnorm_fm(OUT, 2, N, lambda c: mhnorm.t[:, c:c + 1], OUT, c0=2 * h)
        OG = A.alloc([128, KC, N], BF16, nparts=KC)
        TMP = [A.alloc([128, TP], F32) for _ in range(2)]
        for g in range(4):
            wg_ = self.wload(self.wcols("mlstm_w_og", 0, g * 512, 512), [128, KC, 512])
            for hh in range(4):
                cch = g * 4 + hh
                ps = self.ps()
                for c in range(KC):
                    self.mm(ps.t[:, :N], wg_.t[:, c, hh * 128:(hh + 1) * 128], H.t[:, c, :N], c == 0, c == KC - 1, wg_.all + H[c], ps.all)
                t1 = TMP[cch % 2]
                self.sigmoid_from(t1.t[:, :N], ps.t[:, :N], ps.all, t1.all)
                self.tt(OG.t[:, cch, :], OUT.t[:, cch, :], t1.t[:, :N], ALU.mult, OUT[cch] + t1.all, OG[cch])
        self.out_proj("mlstm_w_out", 0, OG, N)
        A.release(m0)

    def gmlp(self, tc, l):
        A, P, X, H = self.A, self.P, self.X, self.H
        N, NSEQ, T, L, NCK = tc.N, tc.NSEQ, tc.T, tc.L, tc.NCK
        samp = tc.kind == "s"
        nrm = self.cst["nrm"]; gbu = self.cst["gbu"]; onesf = self.cst["onesf"]
        self.rmsnorm_fm(X, KC, N, lambda c: nrm.t[:, l, c:c + 1], H)
        m0 = A.mark()
        U = A.alloc([128, KC, N], BF16, nparts=KC)
        vdt = F32 if samp else BF16
        VR = A.alloc([128, NCK, D], vdt, nparts=NCK)
        Vb = A.alloc([128, NCK, D], BF16, nparts=NCK) if samp else VR
        WSM = A.alloc([128, 16, L], BF16)
        BSR = A.alloc([1, 16, L], F32)
        m1 = A.mark()
        WSF = A.alloc([128, 16, L], F32)
        P.dma("sync", WSF.t[:L], self.dram["gwsTs" if samp else "gwsT"], [], WSF.all)
        P.dma("sync", BSR.t, self.dram["gbss" if samp else "gbs"], [], BSR.all)
        self.tt(WSM.t[:L], WSF.t[:L], tc.mask.t[:L, :L].unsqueeze(1).to_broadcast([L, 16, L]), ALU.mult, WSF.all + tc.mask.all, WSM.all)
        A.release(m1)
        for g in range(4):
            w = self.wload(self.wcols("gmlp_w_in", 0, g * 512, 512), [128, KC, 512])
            for hh in range(4):
                c_ = g * 4 + hh
                ps = self.ps()
                for c in range(KC):
                    self.mm(ps.t[:, :N], w.t[:, c, hh * 128:(hh + 1) * 128], H.t[:, c, :N], c == 0, c == KC - 1, w.all + H[c], ps.all)
                self.act(U.t[:, c_, :], ps.t[:, :N], AF.Gelu, ps.all + gbu.all, U[c_], bias=gbu.t[:, c_:c_ + 1])
        m2 = A.mark()
        BT_ = [A.alloc([128, 512], F32) for _ in range(2)]
        TV = [A.alloc([128, 512], F32) for _ in range(2)]
        k = 0
        for g in range(4):
            w = self.wload(self.wcols("gmlp_w_in", 0, D + g * 512, 512), [128, KC, 512])
            bt = BT_[g % 2]
            P.dma("sync", bt.t, self.dram["gbv"][:, g * 512:(g + 1) * 512], [], bt.all)
            for ck in range(NCK):
                ps = self.ps()
                for c in range(KC):
                    self.mm(ps.t[:L, :512], H.t[:, c, ck * L:(ck + 1) * L], w.t[:, c, :], c == 0, c == KC - 1, w.all + H[c], ps.all)
                tv = TV[k % 2]; k += 1
                self.tt(tv.t[:L], ps.t[:L, :512], bt.t[:L], ALU.add, ps.all + bt.all, tv.all)
                self.act(VR.t[:L, ck, g * 512:(g + 1) * 512], tv.t[:L], AF.Gelu, tv.all, VR[ck])
        A.release(m2)
        CEN = A.alloc([128, D], F32)
        SQ = A.alloc([128, D], BF16)
        ST_ = A.alloc([128, 4], F32)
        GB = [A.alloc([128, 512], F32) for _ in range(4)]
        if samp:
            VN = A.alloc([128, D], F32)
        for ck in range(NCK):
            P.op("dve", lambda e, ck=ck: e.reduce_sum(out=ST_.t[:L, 0:1], in_=VR.t[:L, ck, :], axis=mybir.AxisListType.X), VR[ck], ST_.all)
            self.ts(ST_.t[:L, 0:1], ST_.t[:L, 0:1], 1.0 / D, None, ALU.mult, None, ST_.all, ST_.all)
            self.ts(CEN.t[:L], VR.t[:L, ck, :], ST_.t[:L, 0:1], None, ALU.subtract, None, VR[ck] + ST_.all, CEN.all)
            self.tt(SQ.t[:L], CEN.t[:L], CEN.t[:L], ALU.mult, CEN.all, SQ.all)
            P.op("dve", lambda e: e.reduce_sum(out=ST_.t[:L, 1:2], in_=SQ.t[:L], axis=mybir.AxisListType.X), SQ.all, ST_.all)
            self.act(ST_.t[:L, 2:3], ST_.t[:L, 1:2], AF.Ln, ST_.all, ST_.all, scale=1.0 / D, bias=self.epsb.t[:L, 0:1])
            self.act(ST_.t[:L, 2:3], ST_.t[:L, 2:3], AF.Exp, ST_.all, ST_.all, scale=-0.5)
            for b in range(4):
                bs_ = slice(b * 512, (b + 1) * 512)
                gg = GB[(2 * b) % 4]; gb_ = GB[(2 * b + 1) % 4]
                P.dma("sync", gg.t, self.dram["gvg"][:, bs_], [], gg.all)
                P.dma("sync", gb_.t, self.dram["gvb"][:, bs_], [], gb_.all)
                self.stt(CEN.t[:L, bs_], CEN.t[:L, bs_], ST_.t[:L, 2:3], gg.t[:L], ALU.mult, ALU.mult, CEN.all + ST_.all + gg.all, CEN.all)
                if samp:
                    self.tt(VN.t[:L, bs_], CEN.t[:L, bs_], gb_.t[:L], ALU.add, CEN.all + gb_.all, VN.all)
                    self.cp(Vb.t[:L, ck, bs_], VN.t[:L, bs_], VN.all, Vb[ck], eng="act")
                else:
                    self.tt(Vb.t[:L, ck, bs_], CEN.t[:L, bs_], gb_.t[:L], ALU.add, CEN.all + gb_.all + VR[ck], Vb[ck])
        if samp:
            P.dma("sync", self.dram["v_s"], VN.t[:L], VN.all, [], is_output=True)
        for ck in range(NCK):
            cs = slice(ck * L, (ck + 1) * L)
            for g in range(16):
                ps = self.ps()
                self.mm(ps.t[:, :L], Vb.t[:L, ck, g * 128:(g + 1) * 128], WSM.t[:L, g, :], True, False, Vb[ck] + WSM.all, ps.all)
                self.mm(ps.t[:, :L], onesf.t[0:1, :], BSR.t[0:1, g, :], False, True, onesf.all + BSR.all, ps.all)
                self.tt(U.t[:, g, cs], U.t[:, g, cs], ps.t[:, :L], ALU.mult, U[g] + ps.all, U[g])
        self.out_proj("gmlp_w_out", 0, U, N)
        A.release(m0)


def _colvec(v, n):
    v = np.asarray(v, np.float32)
    lead = v.shape[:-1]
    r = v.reshape(lead + (n, 128))
    r = np.moveaxis(r, -1, 0)
    return np.ascontiguousarray(r)


def _consts():
    i = np.arange(128)
    maskp = (i[:, None] <= i[None, :]).astype(np.float32)
    r = np.arange(64)
    masks = ((r[:, None] // 4 == r[None, :] // 4) & (r[:, None] <= r[None, :])).astype(np.float32)
    seq1h = (r[:, None] // 4 == np.arange(16)[None, :]).astype(np.float32)
    ident = np.eye(128, dtype=np.float32)
    ident8 = np.eye(8, dtype=np.float32)
    ones8 = np.ones((8, 128), np.float32)
    sel = np.zeros((8, 8, 128), np.float32)
    for h in range(8):
        sel[h, h, :] = 1.0
    return dict(maskp=maskp, masks=masks, seq1h=seq1h, ident=ident, ident8=ident8, ones8=ones8, sel=sel)


_DEPTH = DEPTH
_TILES = None
_NC_CACHE = {}


def _prep_inputs(inp):
    f = lambda k: np.asarray(inp[k], np.float32)
    shared = {}
    nrm = np.concatenate([f("norm_mix"), f("norm_ffn"), f("norm_final")[None]], axis=0)
    shared["nrm"] = _colvec(nrm, 16)
    shared["hlb"] = _colvec(f("hgrn_lb"), 16)
    shared["honorm"] = _colvec(f("hgrn_onorm"), 16)
    shared["mbif"] = np.ascontiguousarray(f("mlstm_b_if")[0].reshape(2, 8).T)
    shared["mhnorm"] = _colvec(f("mlstm_hnorm")[0], 16)
    shared["mwif"] = np.ascontiguousarray(f("mlstm_w_if")[0])
    bin_ = f("gmlp_b_in")[0]
    shared["gbu"] = _colvec(bin_[:D], 16)
    shared["gbv"] = np.ascontiguousarray(np.broadcast_to(bin_[D:][None, :], (128, D)))
    shared["gvg"] = np.ascontiguousarray(np.broadcast_to(f("gmlp_vnorm_g")[0][None, :], (128, D)))
    shared["gvb"] = np.ascontiguousarray(np.broadcast_to(f("gmlp_vnorm_b")[0][None, :], (128, D)))
    ws = f("gmlp_w_s")[0]
    shared["gwsT"] = np.ascontiguousarray(ws.transpose(2, 0, 1))
    w4 = ws[:, :4, :4].transpose(2, 0, 1)
    shared["gwsTs"] = np.ascontiguousarray(np.tile(w4, (16, 1, 16)))
    bs = f("gmlp_b_s")[0]
    shared["gbs"] = np.ascontiguousarray(bs[None])
    shared["gbss"] = np.ascontiguousarray(np.tile(bs[:, :4], (1, 16))[None])
    cw = np.concatenate([f("ffn_conv_w"), f("ffn_conv_b")[:, None, :]], axis=1)
    shared["cwv"] = _colvec(cw, 88)
    shared.update(_consts())
    for k in ["hgrn_w_q", "hgrn_w_f", "hgrn_w_i", "hgrn_w_g", "hgrn_w_o", "mlstm_w_q", "mlstm_w_k", "mlstm_w_v",
              "mlstm_w_og", "mlstm_w_out", "gmlp_w_in", "gmlp_w_out", "ffn_w_up", "ffn_w_down"]:
        shared[k] = f(k)
    xp = f("x_prompt"); xs = f("x_sample")
    S = f("state_hgrn_S"); C = f("state_mlstm_C"); n = f("state_mlstm_n"); mm_ = f("state_mlstm_m"); cv = f("state_ffn_conv")
    maps = []
    for core in range(8):
        d = dict(shared)
        d["xTp"] = np.ascontiguousarray(xp[core % 4].T)
        sl = slice(core * NS, (core + 1) * NS)
        d["xTs"] = np.ascontiguousarray(xs[sl].reshape(NS * TS, D).T)
        d["S_in"] = np.ascontiguousarray(S[:, sl])
        d["CA_in"] = np.ascontiguousarray(np.concatenate([C[0, sl], n[0, sl][..., None]], axis=-1))
        d["m_in"] = np.ascontiguousarray(mm_[0, sl].T)
        d["conv_in"] = np.ascontiguousarray(cv[:, sl].transpose(0, 3, 1, 2))
        maps.append(d)
    return maps


def _assemble(res):
    R = res
    y_p = np.stack([R[b]["yTp"].T for b in range(4)])
    y_s = np.concatenate([R[c]["yTs"].T.reshape(NS, TS, D) for c in range(8)], axis=0)
    S_p = np.stack([R[b]["S_p"] for b in range(4)], axis=1)
    S_s = np.concatenate([R[c]["S_s"] for c in range(8)], axis=1)
    CA_p = np.stack([R[b]["CA_p"] for b in range(4)])[None]
    CA_s = np.concatenate([R[c]["CA_s"] for c in range(8)], axis=0)[None]
    C_p, n_p = CA_p[..., :256], CA_p[..., 256]
    C_s, n_s = CA_s[..., :256], CA_s[..., 256]
    m_p = np.stack([R[b]["m_p"][:, 0] for b in range(4)])[None]
    m_s = np.concatenate([R[c]["m_s"].T for c in range(8)], axis=0)[None]
    v_s = np.concatenate([R[c]["v_s"].reshape(NS, TS, D) for c in range(8)], axis=0)[None]
    conv_p = np.stack([R[b]["conv_p"].transpose(0, 2, 1) for b in range(4)], axis=1)
    conv_s = np.concatenate([R[c]["conv_s"].transpose(0, 2, 3, 1) for c in range(8)], axis=1)
    outs = (y_p, y_s, S_p, S_s, C_p, C_s, n_p, n_s, m_p, m_s, v_s, conv_p, conv_s)
    return tuple(np.ascontiguousarray(o, dtype=np.float32) for o in outs)


def kernel(**inputs):
    maps = _prep_inputs(inputs)
    key = (_DEPTH, str(_TILES))
    b = Builder(depth=_DEPTH, tiles=_TILES)
    res = run_bass_kernel_spmd(b.nc, maps, core_ids=list(range(8)))
    return _assemble(res.results)
```
